# Optimizing a Trainium2 kernel written in Bass

```python
import jax, jax.numpy as jnp
from jax import lax
import numpy as np

D_MODEL = 2048
BATCH = 8
SEQ = 2048
DEPTH = 1

RET_HEADS = 8
RET_HEAD_DIM = 256
D_RET = RET_HEADS * RET_HEAD_DIM
CONV_GROUPS = 8
D_CONV = D_MODEL
CONV_WIDTH = 3
D_FF = 5632
CHUNK = 128
ROPE_BASE = 10000.0
EPS = 1e-6
N_BRANCHES = 2
IN_COLS = 4 * D_RET + 3 * D_CONV + N_BRANCHES * D_MODEL

kernel_name = "hybrid_retention_shortconv_convffn_adaln"


def _rmsnorm(x, g):
    xf = x.astype(jnp.float32)
    r = xf * lax.rsqrt(jnp.mean(xf * xf, axis=-1, keepdims=True) + EPS)
    return (r * g.astype(jnp.float32)).astype(x.dtype)


def _causal_dwconv(u, w, b=None):
    s = u.shape[1]
    up = jnp.pad(u, ((0, 0), (CONV_WIDTH - 1, 0), (0, 0)))
    y = sum(w[k] * up[:, k:k + s, :] for k in range(CONV_WIDTH))
    if b is not None:
        y = y + b
    return y


def _rotary(x, positions):
    half = x.shape[-1] // 2
    inv_freq = ROPE_BASE ** (-jnp.arange(half, dtype=jnp.float32) / half)
    ang = positions.astype(jnp.float32)[..., None] * inv_freq
    cos = jnp.cos(ang)[:, :, None, :].astype(x.dtype)
    sin = jnp.sin(ang)[:, :, None, :].astype(x.dtype)
    x1, x2 = x[..., :half], x[..., half:]
    return jnp.concatenate([x1 * cos - x2 * sin, x2 * cos + x1 * sin], axis=-1)


def _retention_chunkwise(q, k, v):
    b, s, h, dh = q.shape
    n = s // CHUNK
    dt = q.dtype
    log_gamma = jnp.log(1.0 - 2.0 ** (-5.0 - jnp.arange(h, dtype=jnp.float32)))

    def chunks(t):
        return t.astype(jnp.float32).reshape(b, n, CHUNK, h, dh).transpose(1, 0, 3, 2, 4)

    qc = chunks(q) * (dh ** -0.5)
    kc, vc = chunks(k), chunks(v)

    idx = jnp.arange(CHUNK, dtype=jnp.float32)
    diff = idx[:, None] - idx[None, :]
    decay = jnp.where(diff >= 0,
                      jnp.exp(log_gamma[:, None, None] * jnp.maximum(diff, 0.0)),
                      0.0)
    scores = jnp.einsum('nbhid,nbhjd->nbhij', qc, kc) * decay
    y_inner = jnp.einsum('nbhij,nbhjd->nbhid', scores, vc)

    q_dec = jnp.exp(log_gamma[:, None] * (idx + 1.0))[None, :, :, None]
    k_dec = jnp.exp(log_gamma[:, None] * (CHUNK - 1.0 - idx))[None, :, :, None]
    chunk_dec = jnp.exp(log_gamma * CHUNK)[None, :, None, None]

    def step(state, xs):
        qn, kn, vn = xs
        y_cross = jnp.einsum('bhid,bhde->bhie', qn * q_dec, state)
        state = chunk_dec * state + jnp.einsum('bhjd,bhje->bhde', kn * k_dec, vn)
        return state, y_cross

    state0 = jnp.zeros((b, h, dh, dh), jnp.float32)
    _, y_cross = lax.scan(step, state0, (qc, kc, vc))
    y = (y_inner + y_cross).transpose(1, 0, 3, 2, 4).reshape(b, s, h, dh)
    y = y * lax.rsqrt(jnp.mean(y * y, axis=-1, keepdims=True) + EPS)
    return y.astype(dt)


def setup_inputs(seed: int = 0) -> dict:
    key = jax.random.key(seed)
    ks = jax.random.split(key, 20)
    f32 = jnp.float32

    def nrm(k, shape, fan_in):
        return jax.random.normal(k, shape, f32) * (fan_in ** -0.5)

    x = jax.random.normal(ks[0], (BATCH, SEQ, D_MODEL), f32)
    c = jax.random.normal(ks[1], (BATCH, D_MODEL), f32)
    positions = jnp.broadcast_to(jnp.arange(SEQ, dtype=jnp.int32)[None, :], (BATCH, SEQ))
    return {
        "x": x,
        "c": c,
        "positions": positions,
        "norm1_g": 1.0 + 0.02 * jax.random.normal(ks[2], (DEPTH, D_MODEL), f32),
        "norm2_g": 1.0 + 0.02 * jax.random.normal(ks[3], (DEPTH, D_MODEL), f32),
        "w_ada": nrm(ks[4], (DEPTH, D_MODEL, 6 * D_MODEL), D_MODEL),
        "b_ada": 0.02 * jax.random.normal(ks[5], (DEPTH, 6 * D_MODEL), f32),
        "w_in": nrm(ks[6], (DEPTH, D_MODEL, IN_COLS), D_MODEL),
        "b_gate": 0.02 * jax.random.normal(ks[7], (DEPTH, N_BRANCHES * D_MODEL), f32),
        "w_sc": nrm(ks[8], (DEPTH, CONV_WIDTH, D_CONV), CONV_WIDTH),
        "w_ret_o": nrm(ks[9], (DEPTH, D_RET, D_MODEL), D_RET),
        "w_conv_o": nrm(ks[10], (DEPTH, D_CONV, D_MODEL), D_CONV),
        "w_mix_o": nrm(ks[11], (DEPTH, D_MODEL, D_MODEL), D_MODEL),
        "w_up": nrm(ks[12], (DEPTH, D_MODEL, D_FF), D_MODEL),
        "w_gate": nrm(ks[13], (DEPTH, D_MODEL, D_FF), D_MODEL),
        "w_ffconv": nrm(ks[14], (DEPTH, CONV_WIDTH, D_FF), CONV_WIDTH),
        "b_ffconv": 0.02 * jax.random.normal(ks[15], (DEPTH, D_FF), f32),
        "w_down": nrm(ks[16], (DEPTH, D_FF, D_MODEL), D_FF),
        "final_g": 1.0 + 0.02 * jax.random.normal(ks[17], (D_MODEL,), f32),
    }


def reference(x, c, positions, norm1_g, norm2_g, w_ada, b_ada, w_in, b_gate, w_sc,
              w_ret_o, w_conv_o, w_mix_o, w_up, w_gate, w_ffconv, b_ffconv, w_down,
              final_g):
    b, s, _ = x.shape
    h = x
    c_act = jax.nn.silu(c)
    for l in range(DEPTH):
        mod = c_act @ w_ada[l] + b_ada[l]
        sh1, sc1, g1, sh2, sc2, g2 = [m[:, None, :] for m in jnp.split(mod, 6, axis=-1)]

        xn = _rmsnorm(h, norm1_g[l]) * (1.0 + sc1) + sh1
        proj = xn @ w_in[l]
        splits = np.cumsum([D_RET, D_RET, D_RET, D_RET, D_CONV, D_CONV, D_CONV])
        q, k, v, g_ret, bg, cg, xc, gate_logits = jnp.split(proj, splits, axis=-1)

        q = _rotary(q.reshape(b, s, RET_HEADS, RET_HEAD_DIM), positions)
        k = _rotary(k.reshape(b, s, RET_HEADS, RET_HEAD_DIM), positions)
        v = v.reshape(b, s, RET_HEADS, RET_HEAD_DIM)
        y_ret = _retention_chunkwise(q, k, v).reshape(b, s, D_RET)
        y_ret = (y_ret * jax.nn.silu(g_ret)) @ w_ret_o[l]

        y_conv = bg * _causal_dwconv(cg * xc, w_sc[l])
        y_conv = y_conv @ w_conv_o[l]

        gates = jax.nn.sigmoid(gate_logits + b_gate[l])
        gate_a, gate_b = jnp.split(gates, N_BRANCHES, axis=-1)
        mixed = (gate_a * y_ret + gate_b * y_conv) @ w_mix_o[l]
        h = h + g1 * mixed

        xn2 = _rmsnorm(h, norm2_g[l]) * (1.0 + sc2) + sh2
        up = xn2 @ w_up[l]
        gt = _causal_dwconv(xn2 @ w_gate[l], w_ffconv[l], b_ffconv[l])
        ff = (jax.nn.silu(gt) * up) @ w_down[l]
        h = h + g2 * ff
    return _rmsnorm(h, final_g)
```

```python
import numpy as np
import concourse.bass as bass
import concourse.mybir as mybir
from concourse.bass_utils import run_bass_kernel_spmd

F32 = mybir.dt.float32
BF16 = mybir.dt.bfloat16
I32 = mybir.dt.int32
AF = mybir.ActivationFunctionType
ALU = mybir.AluOpType

D = 2048
S = 2048
T = 512
NT = S // T
NCH = T // 128
KC = 16
DFF = 5632
FC = DFF // 128
H = 8
DH = 256
IN_COLS = 18432
EPS = 1e-6
NW = 4
WB = 256
MAGIC = 12582912.0
TWO_PI = float(2.0 * np.pi)

C_C, C_N1, C_N2, C_BADA, C_BG, C_WSC, C_WFF, C_BFF, C_KDEC, C_EPSQ, C_INVF = 0, 16, 32, 48, 144, 176, 224, 356, 400, 408, 416
NCOLS = 417

LOG_GAMMA = np.log(1.0 - 2.0 ** (-5.0 - np.arange(H, dtype=np.float64)))
G128 = [float(np.exp(LOG_GAMMA[h] * 128.0)) for h in range(H)]


class Sem:
    def __init__(self, nc, name):
        self.h = nc.alloc_semaphore(name)
        self.count = 0


class Eng:
    def __init__(self, nc, name, eng, is_pe=False):
        self.e = eng
        self.sem = Sem(nc, "s_" + name)
        self.waited = {}
        self.is_pe = is_pe


class Buf:
    __slots__ = ("w", "r", "name")

    def __init__(self, name=""):
        self.w = None
        self.r = {}
        self.name = name


def _deps(E, reads, writes):
    need = {}
    for b in reads:
        if b.w is not None:
            s, c = b.w
            if need.get(s, 0) < c:
                need[s] = c
    for b in writes:
        if b.w is not None:
            s, c = b.w
            if need.get(s, 0) < c:
                need[s] = c
        for s, c in b.r.items():
            if need.get(s, 0) < c:
                need[s] = c
    for s, c in need.items():
        if s is E.sem and E.is_pe:
            continue
        if E.waited.get(s, 0) >= c:
            continue
        E.e.wait_ge(s.h, c)
        E.waited[s] = c


def _commit(ev, reads, writes):
    s, c = ev
    for b in reads:
        if b.r.get(s, 0) < c:
            b.r[s] = c
    for b in writes:
        b.w = ev
        b.r = {}


def emit(E, reads, writes, fn):
    _deps(E, reads, writes)
    ins = fn()
    E.sem.count += 1
    ins.then_inc(E.sem.h, 1)
    _commit((E.sem, E.sem.count), reads, writes)
    return ins


def dma(Q, sem, out_ap, in_ap, reads, writes):
    _deps(Q, reads, writes)
    ins = Q.e.dma_start(out=out_ap, in_=in_ap)
    sem.count += 16
    ins.then_inc(sem.h, 16)
    _commit((sem, sem.count), reads, writes)
    return ins


def gen_blocks(nt):
    out = []
    for b in range(48):
        out.append(("ada", b * WB, 0, 16))
    for t in range(nt):
        for h in range(H):
            for base in (0, 2048, 4096, 6144):
                out.append(("in", base + h * WB, 0, 16))
        for cp in range(8):
            for base in (10240, 12288, 8192):
                out.append(("in", base + cp * WB, 0, 16))
        for cp in range(8):
            out.append(("in", 14336 + cp * WB, 0, 16))
            out.append(("in", 16384 + cp * WB, 0, 16))
            out.append(("ro", cp * WB, 0, 16))
            out.append(("co", cp * WB, 0, 16))
        for cp in range(8):
            out.append(("mo", cp * WB, 0, 16))
        for fp in range(FC // 2):
            out.append(("gate", fp * WB, 0, 16))
            out.append(("up", fp * WB, 0, 16))
        for cp in range(8):
            out.append(("down", cp * WB, 0, 16))
            out.append(("down", cp * WB, 16, 16))
            out.append(("down", cp * WB, 32, 12))
    return out


def build_nc(nt=NT, debug=None):
    nc = bass.Bass("TRN2", target_bir_lowering=False)

    def din(name, shape, dt=F32):
        return nc.dram_tensor(name, list(shape), dt, kind="ExternalInput").ap()

    x_d = din("x", [S, D])
    pos_d = din("pos", [1, S], I32)
    cols_d = din("cols", [128, NCOLS])
    fg_d = din("fg", [1, D])
    ident_d = din("ident", [128, 128])
    mask_d = din("maskT", [128, 128])
    wd = {
        "ada": din("w_ada", [D, 6 * D]),
        "in": din("w_in", [D, IN_COLS]),
        "ro": din("w_ret_o", [D, D]),
        "co": din("w_conv_o", [D, D]),
        "mo": din("w_mix_o", [D, D]),
        "up": din("w_up", [D, DFF]),
        "gate": din("w_gate", [D, DFF]),
        "down": din("w_down", [DFF, D]),
    }
    wv = {k: v.rearrange("(kc p) n -> p kc n", p=128) for k, v in wd.items()}
    y_d = nc.dram_tensor("y", [S, D], F32, kind="ExternalOutput").ap()

    PE = Eng(nc, "pe", nc.tensor, is_pe=True)
    ACT = Eng(nc, "act", nc.scalar)
    DVE = Eng(nc, "dve", nc.vector)
    POOL = Eng(nc, "pool", nc.gpsimd)
    SP = Eng(nc, "sp", nc.sync)
    cst = Sem(nc, "cst")
    cst2 = Sem(nc, "cst2")
    possem = Sem(nc, "possem")
    wsem = [Sem(nc, f"wsem{i}") for i in range(NW)]
    hio = [Sem(nc, f"hio{i}") for i in range(NCH)]

    def sb(name, shape, dt=F32):
        return nc.alloc_sbuf_tensor("sb_" + name, list(shape), dt)

    wslot = [sb(f"wslot{i}", [128, 16, WB], BF16) for i in range(NW)]
    wslotB = [Buf(f"w{i}") for i in range(NW)]
    h_t = sb("h", [128, NCH, D], F32)
    hB = [Buf(f"h{n}") for n in range(NCH)]
    xnT = sb("xnT", [128, KC, T], BF16)
    xnB = [Buf(f"xn{c}") for c in range(KC)]
    act = sb("act", [128, 48, T], BF16)
    actB = [Buf(f"act{i}") for i in range(48)]
    rt = act[:, 0:16, :].rearrange("p a t -> p (a t)").rearrange("p (n f) -> p n f", n=NCH)
    fgbc = sb("fgbc", [128, D], F32)
    fgB = Buf("fg")
    shat = sb("shat", [128, H, 2, DH], F32)
    shatB = [Buf(f"shat{h}") for h in range(H)]
    cols = sb("cols", [128, NCOLS], F32)
    colsB = Buf("cols")
    identf = sb("identf", [128, 128], F32)
    identb = sb("identb", [128, 128], BF16)
    maskT = sb("maskT", [128, 128], F32)
    identfB, identbB, maskB = Buf(), Buf(), Buf()
    modcol = sb("modcol", [128, 96], F32)
    modB = Buf("mod")
    a1col = sb("a1col", [128, 16], F32)
    a2col = sb("a2col", [128, 16], F32)
    cact = sb("cact", [128, 16], BF16)
    cactB = Buf("cact")
    one11 = sb("one11", [1, 1], F32)
    oneB = Buf("one")
    halo_sc = sb("halo_sc", [128, 16, 2], F32)
    halo_ff = sb("halo_ff", [128, FC, 2], F32)
    halo_scB = Buf("halo_sc")
    halo_ffB = Buf("halo_ff")
    cosT = sb("cosT", [128, T], F32)
    sinT = sb("sinT", [128, T], F32)
    cosB, sinB = Buf("cos"), Buf("sin")
    posi = sb("posi", [128, T], I32)
    posB = Buf("pos")
    NSCR = 6
    scr_t = [sb(f"scr{i}", [128, T], F32) for i in range(NSCR)]
    scr_B = [Buf(f"scr{i}") for i in range(NSCR)]
    scr_i = [0]

    def scr():
        i = scr_i[0] % NSCR
        scr_i[0] += 1
        return scr_t[i], scr_B[i]

    pbuf_t = [sb(f"pbuf{i}", [128, T + 2], F32) for i in range(2)]
    pbuf_B = [Buf(), Buf()]
    pbuf_i = [0]

    def pbuf():
        i = pbuf_i[0] % 2
        pbuf_i[0] += 1
        return pbuf_t[i], pbuf_B[i]

    qT = [sb(f"qT{i}", [128, 2, T], BF16) for i in range(2)]
    kT = [sb(f"kT{i}", [128, 2, T], BF16) for i in range(2)]
    ktok = [sb(f"ktok{i}", [128, NCH, DH], BF16) for i in range(2)]
    vtok = [sb(f"vtok{i}", [128, NCH, DH], BF16) for i in range(2)]
    sgtok = [sb(f"sgtok{i}", [128, NCH, DH], BF16) for i in range(2)]
    qTB = [Buf(), Buf()]
    kTB = [Buf(), Buf()]
    ktokB = [Buf(), Buf()]
    vtokB = [Buf(), Buf()]
    sgtokB = [Buf(), Buf()]
    ftmp = [sb(f"ftmp{i}", [128, 2, T], BF16) for i in range(2)]
    ftmpB = [Buf(), Buf()]
    ftmp_i = [0]
    stbf = [sb(f"stbf{i}", [128, 2, DH], BF16) for i in range(2)]
    stbfB = [Buf(), Buf()]
    STt = [sb(f"ST{i}", [128, 128], BF16) for i in range(2)]
    STB = [Buf(), Buf()]
    ygt = [sb(f"ygt{i}", [128, DH], BF16) for i in range(2)]
    ygtB = [Buf(), Buf()]
    junk = sb("junk", [128, DH], BF16)
    junkB = Buf("junk")
    sm_t = sb("smallcols", [128, 64], F32)
    sm_B = [Buf(f"sm{i}") for i in range(64)]
    sm_i = [0]

    def smcol(n=1):
        i = sm_i[0]
        if (i % 64) + n > 64:
            i += 64 - (i % 64)
        sm_i[0] = i + n
        j = i % 64
        return sm_t[:, j:j + n], sm_B[j:j + n]

    mainb = [nc.alloc_psum_tensor(f"pm{i}", [128, T], F32) for i in range(4)]
    mainB = [Buf(f"pm{i}") for i in range(4)]
    smb = [nc.alloc_psum_tensor(f"ps{i}", [128, T], F32) for i in range(2)]
    smB = [Buf(f"ps{i}") for i in range(2)]
    tbb = [nc.alloc_psum_tensor(f"pt{i}", [128, 2 * T], BF16) for i in range(2)]
    tbB = [Buf(f"pt{i}") for i in range(2)]
    ring = {"m": 0, "s": 0, "t": 0}

    def bank(kind):
        lst, bl = {"m": (mainb, mainB), "s": (smb, smB), "t": (tbb, tbB)}[kind]
        i = ring[kind] % len(lst)
        ring[kind] += 1
        return lst[i], bl[i]

    blocks = gen_blocks(nt)
    wstate = {"issued": 0, "used": 0}

    def w_issue(j):
        name, col0, kc0, nkc = blocks[j]
        s = j % NW
        dma(POOL, wsem[s], wslot[s][:, 0:nkc, :], wv[name][:, kc0:kc0 + nkc, col0:col0 + WB], [], [wslotB[s]])

    def w_get(expect):
        j = wstate["used"]
        assert blocks[j] == expect, (blocks[j], expect)
        while wstate["issued"] < min(len(blocks), j + NW):
            w_issue(wstate["issued"])
            wstate["issued"] += 1
        wstate["used"] += 1
        s = j % NW
        return wslot[s], wslotB[s]

    def mm_group(bk, bkB, out_ap, pairs, reads, start=True, stop=True):
        def fn():
            ins = None
            n = len(pairs)
            for i, (l, r) in enumerate(pairs):
                ins = nc.tensor.matmul(out_ap, lhsT=l, rhs=r, start=(start and i == 0), stop=(stop and i == n - 1),
                                       skip_group_check=True)
            return ins
        return emit(PE, reads, [bkB], fn)

    def transposes(bkB, items, ident, reads):
        def fn():
            ins = None
            for o, i_ in items:
                ins = nc.tensor.transpose(o, i_, ident)
            return ins
        return emit(PE, reads, [bkB], fn)

    def act_op(reads, writes, **kw):
        return emit(ACT, reads, writes, lambda: nc.scalar.activation(**kw))

    def V(E):
        return E.e

    def tt(E, reads, writes, out, in0, in1, op):
        return emit(E, reads, writes, lambda: V(E).tensor_tensor(out=out, in0=in0, in1=in1, op=op))

    def ts(E, reads, writes, out, in0, s1, s2, op0, op1=None):
        if op1 is None:
            return emit(E, reads, writes, lambda: V(E).tensor_scalar(out=out, in0=in0, scalar1=s1, scalar2=None, op0=op0))
        return emit(E, reads, writes, lambda: V(E).tensor_scalar(out=out, in0=in0, scalar1=s1, scalar2=s2, op0=op0, op1=op1))

    def stt(E, reads, writes, out, in0, scalar, in1, op0, op1):
        return emit(E, reads, writes, lambda: V(E).scalar_tensor_tensor(out=out, in0=in0, scalar=scalar, in1=in1, op0=op0, op1=op1))

    def cp(E, reads, writes, out, in_):
        return emit(E, reads, writes, lambda: V(E).tensor_copy(out=out, in_=in_))

    dma(SP, cst, cols[:], cols_d, [], [colsB])
    dma(SP, cst, fgbc[:], fg_d.partition_broadcast(128), [], [fgB])
    dma(SP, cst, identf[:], ident_d, [], [identfB])
    dma(SP, cst, maskT[:], mask_d, [], [maskB])
    for b in (colsB, fgB, identfB, maskB):
        b.w = (cst, cst.count)
    dma(POOL, cst2, identb[:], ident_d, [], [identbB])
    emit(POOL, [], [oneB], lambda: nc.gpsimd.memset(one11[:], 1.0))
    emit(POOL, [], shatB, lambda: nc.gpsimd.memset(shat[:].rearrange("p h c e -> p (h c e)"), 0.0))
    emit(POOL, [], [halo_scB], lambda: nc.gpsimd.memset(halo_sc[:].rearrange("p c k -> p (c k)"), 0.0))
    emit(POOL, [], [halo_ffB], lambda: nc.gpsimd.memset(halo_ff[:].rearrange("p c k -> p (c k)"), 0.0))

    act_op([colsB], [cactB], out=cact[:], in_=cols[:, C_C:C_C + 16], func=AF.Silu)
    pcol_t, pcolB = smb[0], smB[0]
    ring["s"] = 1
    for b in range(48):
        slot, sB = w_get(("ada", b * WB, 0, 16))
        bk, bkB = bank("m")
        mm_group(bk, bkB, bk[0:1, 0:WB], [(cact[:, kc:kc + 1], slot[:, kc, :]) for kc in range(16)], [sB, cactB])
        rb, rB = scr()
        act_op([bkB], [rB], out=rb[0:1, 0:WB], in_=bk[0:1, 0:WB], func=AF.Copy)

        def fn(b=b, rb=rb):
            ins = None
            for j in range(2):
                ins = nc.tensor.matmul(pcol_t[:, 2 * b + j:2 * b + j + 1], lhsT=rb[0:1, j * 128:(j + 1) * 128],
                                       rhs=one11[0:1, 0:1], start=True, stop=True, skip_group_check=True)
            return ins
        emit(PE, [rB, oneB], [pcolB], fn)
    tt(DVE, [pcolB, colsB], [modB], modcol[:], pcol_t[:, 0:96], cols[:, C_BADA:C_BADA + 96], ALU.add)
    stt(DVE, [modB, colsB], [modB], a1col[:], modcol[:, 16:32], 1.0, cols[:, C_N1:C_N1 + 16], ALU.add, ALU.mult)
    stt(DVE, [modB, colsB], [modB], a2col[:], modcol[:, 64:80], 1.0, cols[:, C_N2:C_N2 + 16], ALU.add, ALU.mult)
    sh1col, g1col, sh2col, g2col = modcol[:, 0:16], modcol[:, 32:48], modcol[:, 48:64], modcol[:, 80:96]

    def rmsnorm_to_xnT(acol, shcol):
        ssc, ssB_ = smcol(4)
        rsc, rsB_ = smcol(4)
        rstd, rstdB_ = smcol(4)
        for n in range(NCH):
            sl = actB[4 * n:4 * n + 4]
            act_op([hB[n]], sl + [ssB_[n]], out=rt[:, n, :], in_=h_t[:, n, :], func=AF.Square, accum_out=ssc[:, n:n + 1])
            act_op([ssB_[n]], [rsB_[n]], out=rsc[:, n:n + 1], in_=ssc[:, n:n + 1], func=AF.Sqrt, scale=1.0 / D, bias=EPS)
            emit(DVE, [rsB_[n]], [rstdB_[n]], lambda n=n: nc.vector.reciprocal(out=rstd[:, n:n + 1], in_=rsc[:, n:n + 1]))
            ts(DVE, [hB[n], rstdB_[n]], sl, rt[:, n, :], h_t[:, n, :], rstd[:, n:n + 1], None, ALU.mult)
        for c in range(KC):
            bk, bkB = bank("t")
            v = bk[:, 0:T].rearrange("p (n t) -> p n t", n=NCH)
            transposes(bkB, [(v[:, n, :], rt[:, n, c * 128:(c + 1) * 128]) for n in range(NCH)], identb[:],
                       actB[0:16] + [identbB])
            act_op([bkB, modB], [xnB[c]], out=xnT[:, c, :], in_=bk[:, 0:T], func=AF.Identity,
                   scale=acol[:, c:c + 1], bias=shcol[:, c:c + 1])

    def proj2(slot, sB, src, srcB):
        res = []
        for oc in range(2):
            bk, bkB = bank("m")
            mm_group(bk, bkB, bk[:, :], [(slot[:, kc, oc * 128:(oc + 1) * 128], src[:, kc, :]) for kc in range(KC)],
                     [sB] + srcB)
            res.append((bk, bkB))
        return res

    def rotary(banks, dst, dstB):
        (x1, x1B), (x2, x2B) = banks
        t1, t1B = scr()
        t2, t2B = scr()
        tt(DVE, [x1B, cosB], [t1B], t1[:], x1[:, :], cosT[:], ALU.mult)
        tt(DVE, [x2B, sinB], [t2B], t2[:], x2[:, :], sinT[:], ALU.mult)
        tt(POOL, [t1B, t2B], [dstB], dst[:, 0, :], t1[:], t2[:], ALU.subtract)
        t3, t3B = scr()
        t4, t4B = scr()
        tt(DVE, [x2B, cosB], [t3B], t3[:], x2[:, :], cosT[:], ALU.mult)
        tt(DVE, [x1B, sinB], [t4B], t4[:], x1[:, :], sinT[:], ALU.mult)
        tt(POOL, [t3B, t4B], [dstB], dst[:, 1, :], t3[:], t4[:], ALU.add)

    def to_tokmajor(src, srcB, dst, dstB, scale_ap=None, scale_reads=()):
        bk, bkB = bank("t")
        v = bk[:, :].rearrange("p (n c e) -> p n c e", n=NCH, c=2)
        transposes(bkB, [(v[:, n, c, :], src[:, c, n * 128:(n + 1) * 128]) for n in range(NCH) for c in range(2)],
                   identb[:], [srcB, identbB])
        if scale_ap is None:
            act_op([bkB], [dstB], out=dst[:].rearrange("p n e -> p (n e)"), in_=bk[:, :], func=AF.Copy)
        else:
            act_op([bkB] + list(scale_reads), [dstB], out=dst[:].rearrange("p n e -> p (n e)"), in_=bk[:, :],
                   func=AF.Identity, scale=scale_ap)

    def retention(hh, r):
        for n in range(NCH):
            tok = slice(n * 128, (n + 1) * 128)
            sb_t, sbB = stbf[n % 2], stbfB[n % 2]
            act_op([shatB[hh]], [sbB], out=sb_t[:].rearrange("p c e -> p (c e)"),
                   in_=shat[:, hh, :, :].rearrange("p c e -> p (c e)"), func=AF.Identity, scale=G128[hh])
            bk, bkB = bank("s")
            mm_group(bk, bkB, bk[:, 0:128], [(kT[r][:, dc, tok], qT[r][:, dc, tok]) for dc in range(2)], [kTB[r], qTB[r]])
            st, stB = STt[n % 2], STB[n % 2]
            stt(DVE, [bkB, colsB, maskB], [stB], st[:], bk[:, 0:128], cols[:, C_KDEC + hh:C_KDEC + hh + 1], maskT[:],
                ALU.mult, ALU.mult)
            yk, ykB = bank("s")
            pairs = [(st[:], vtok[r][:, n, :])] + [(qT[r][:, dc, tok], sb_t[:, dc, :]) for dc in range(2)]
            mm_group(yk, ykB, yk[:, 0:DH], pairs, [stB, vtokB[r], qTB[r], sbB])
            uk, ukB = bank("s")

            def fnU(uk=uk, n=n):
                ins = None
                for dc in range(2):
                    ins = nc.tensor.matmul(uk[:, dc * DH:(dc + 1) * DH], lhsT=ktok[r][:, n, dc * 128:(dc + 1) * 128],
                                           rhs=vtok[r][:, n, :], start=True, stop=True, skip_group_check=True)
                return ins
            emit(PE, [ktokB[r], vtokB[r]], [ukB], fnU)
            sh_v = shat[:, hh, :, :].rearrange("p c e -> p (c e)")
            stt(DVE, [ukB, shatB[hh]], [shatB[hh]], sh_v, sh_v, G128[hh], uk[:, 0:2 * DH], ALU.mult, ALU.add)
            ss, ssB_ = smcol(1)
            rs, rsB_ = smcol(1)
            rstd, rstdB_ = smcol(1)
            act_op([ykB], [junkB, ssB_[0]], out=junk[:], in_=yk[:, 0:DH], func=AF.Square, accum_out=ss)
            act_op([ssB_[0], colsB], [rsB_[0]], out=rs, in_=ss, func=AF.Sqrt, scale=1.0 / DH,
                   bias=cols[:, C_EPSQ + hh:C_EPSQ + hh + 1])
            emit(DVE, [rsB_[0]], [rstdB_[0]], lambda rs=rs, rstd=rstd: nc.vector.reciprocal(out=rstd, in_=rs))
            yg, ygB = ygt[n % 2], ygtB[n % 2]
            stt(DVE, [ykB, rstdB_[0], sgtokB[r]], [ygB], yg[:], yk[:, 0:DH], rstd, sgtok[r][:, n, :], ALU.mult, ALU.mult)
            tk, tkB = bank("t")
            tv = tk[:, 0:256].rearrange("p (c t) -> p c t", c=2)
            transposes(tkB, [(tv[:, ec, :], yg[:, ec * 128:(ec + 1) * 128]) for ec in range(2)], identb[:], [ygB, identbB])
            act_op([tkB], [actB[2 * hh], actB[2 * hh + 1]], out=act[:, 2 * hh:2 * hh + 2, tok], in_=tv, func=AF.Copy)

    def out_proj_residual(specs, src_kc_list, c0, gcol):
        bks = [bank("m"), bank("m")]
        nb = len(specs)
        for j, (spec, kcs) in enumerate(zip(specs, src_kc_list)):
            slot, sB = w_get(spec)
            for oc in range(2):
                bk, bkB = bks[oc]
                mm_group(bk, bkB, bk[:, :],
                         [(slot[:, i, oc * 128:(oc + 1) * 128], act[:, kc, :]) for i, kc in enumerate(kcs)],
                         [sB] + [actB[kc] for kc in kcs], start=(j == 0), stop=(j == nb - 1))
        for oc in range(2):
            c = c0 + oc
            bk, bkB = bks[oc]
            tmp, tmpB = scr()
            act_op([bkB, modB], [tmpB], out=tmp[:], in_=bk[:, :], func=AF.Identity, scale=gcol[:, c:c + 1])
            sk, skB = bank("s")
            sv = sk[:, :].rearrange("p (n f) -> p n f", n=NCH)
            transposes(skB, [(sv[:, n, :], tmp[:, n * 128:(n + 1) * 128]) for n in range(NCH)], identf[:], [tmpB, identfB])
            hv = h_t[:, :, c * 128:(c + 1) * 128]
            tt(DVE, [skB] + hB, hB, hv, hv, sv, ALU.add)

    dbg_out = {}

    for t in range(nt):
        t0 = t * T
        for n in range(NCH):
            dma(SP, hio[n], h_t[:, n, :], x_d[t0 + n * 128:t0 + (n + 1) * 128, :], [], [hB[n]])
        dma(SP, possem, posi[:], pos_d[0:1, t0:t0 + T].partition_broadcast(128), [], [posB])
        ang, angB = scr()
        cp(DVE, [posB], [angB], ang[:], posi[:])
        ts(DVE, [angB, colsB], [angB], ang[:], ang[:], cols[:, C_INVF:C_INVF + 1], None, ALU.mult)
        for (dst, dstB, shift) in ((sinT, sinB, 0.0), (cosT, cosB, float(np.pi / 2))):
            a2, a2B = scr()
            if shift != 0.0:
                ts(DVE, [angB], [a2B], a2[:], ang[:], shift, None, ALU.add)
                src, srcB = a2, a2B
            else:
                src, srcB = ang, angB
            k_, kB_ = scr()
            ts(DVE, [srcB], [kB_], k_[:], src[:], 1.0 / TWO_PI, MAGIC, ALU.mult, ALU.add)
            ts(DVE, [kB_], [kB_], k_[:], k_[:], MAGIC, -TWO_PI, ALU.subtract, ALU.mult)
            tt(DVE, [srcB, kB_], [kB_], k_[:], src[:], k_[:], ALU.add)
            ts(DVE, [kB_], [kB_], k_[:], k_[:], 3.14159, -3.14159, ALU.min, ALU.max)
            act_op([kB_], [dstB], out=dst[:], in_=k_[:], func=AF.Sin)

        rmsnorm_to_xnT(a1col, sh1col)
        if debug == "xn" and t == 0:
            dbg_out["xnT"] = (xnT, xnB)

        for hh in range(H):
            r = hh % 2
            slot, sB = w_get(("in", hh * WB, 0, 16))
            rotary(proj2(slot, sB, xnT, xnB), qT[r], qTB[r])
            slot, sB = w_get(("in", 2048 + hh * WB, 0, 16))
            rotary(proj2(slot, sB, xnT, xnB), kT[r], kTB[r])
            to_tokmajor(kT[r], kTB[r], ktok[r], ktokB[r], scale_ap=cols[:, C_KDEC + hh:C_KDEC + hh + 1], scale_reads=[colsB])
            slot, sB = w_get(("in", 4096 + hh * WB, 0, 16))
            bks = proj2(slot, sB, xnT, xnB)
            fi = ftmp_i[0] % 2
            ftmp_i[0] += 1
            for oc in range(2):
                act_op([bks[oc][1]], [ftmpB[fi]], out=ftmp[fi][:, oc, :], in_=bks[oc][0][:, :], func=AF.Copy)
            to_tokmajor(ftmp[fi], ftmpB[fi], vtok[r], vtokB[r])
            slot, sB = w_get(("in", 6144 + hh * WB, 0, 16))
            bks = proj2(slot, sB, xnT, xnB)
            fi = ftmp_i[0] % 2
            ftmp_i[0] += 1
            for oc in range(2):
                act_op([bks[oc][1]], [ftmpB[fi]], out=ftmp[fi][:, oc, :], in_=bks[oc][0][:, :], func=AF.Silu)
            to_tokmajor(ftmp[fi], ftmpB[fi], sgtok[r], sgtokB[r])
            retention(hh, r)

        for cpair in range(8):
            slot, sB = w_get(("in", 10240 + cpair * WB, 0, 16))
            cg = proj2(slot, sB, xnT, xnB)
            cgs = []
            for oc in range(2):
                s_, sB_ = scr()
                act_op([cg[oc][1]], [sB_], out=s_[:], in_=cg[oc][0][:, :], func=AF.Copy)
                cgs.append((s_, sB_))
            slot, sB = w_get(("in", 12288 + cpair * WB, 0, 16))
            xc = proj2(slot, sB, xnT, xnB)
            ts_ = []
            for oc in range(2):
                c = 2 * cpair + oc
                p, pB = pbuf()
                cp(POOL, [halo_scB], [pB], p[:, 0:2], halo_sc[:, c, :])
                tt(DVE, [xc[oc][1], cgs[oc][1]], [pB], p[:, 2:T + 2], xc[oc][0][:, :], cgs[oc][0][:], ALU.mult)
                cp(POOL, [pB], [halo_scB], halo_sc[:, c, :], p[:, T:T + 2])
                tq, tqB = cgs[oc]
                act_op([pB, colsB], [tqB], out=tq[:], in_=p[:, 2:T + 2], func=AF.Identity,
                       scale=cols[:, C_WSC + 32 + c:C_WSC + 32 + c + 1])
                stt(DVE, [pB, colsB, tqB], [tqB], tq[:], p[:, 1:T + 1], cols[:, C_WSC + 16 + c:C_WSC + 16 + c + 1], tq[:],
                    ALU.mult, ALU.add)
                stt(DVE, [pB, colsB, tqB], [tqB], tq[:], p[:, 0:T], cols[:, C_WSC + c:C_WSC + c + 1], tq[:],
                    ALU.mult, ALU.add)
                ts_.append((tq, tqB))
            slot, sB = w_get(("in", 8192 + cpair * WB, 0, 16))
            bg = proj2(slot, sB, xnT, xnB)
            for oc in range(2):
                c = 2 * cpair + oc
                tt(DVE, [bg[oc][1], ts_[oc][1]], [actB[16 + c]], act[:, 16 + c, :], bg[oc][0][:, :], ts_[oc][0][:], ALU.mult)

        for cpair in range(8):
            slot, sB = w_get(("in", 14336 + cpair * WB, 0, 16))
            la = proj2(slot, sB, xnT, xnB)
            ga = []
            for oc in range(2):
                c = 2 * cpair + oc
                s_, sB_ = scr()
                act_op([la[oc][1], colsB], [sB_], out=s_[:], in_=la[oc][0][:, :], func=AF.Sigmoid,
                       bias=cols[:, C_BG + c:C_BG + c + 1])
                ga.append((s_, sB_))
            slot, sB = w_get(("in", 16384 + cpair * WB, 0, 16))
            lb = proj2(slot, sB, xnT, xnB)
            gb = []
            for oc in range(2):
                c = 2 * cpair + oc
                s_, sB_ = scr()
                act_op([lb[oc][1], colsB], [sB_], out=s_[:], in_=lb[oc][0][:, :], func=AF.Sigmoid,
                       bias=cols[:, C_BG + 16 + c:C_BG + 16 + c + 1])
                gb.append((s_, sB_))
            slot, sB = w_get(("ro", cpair * WB, 0, 16))
            ro = proj2(slot, sB, act, actB[0:16])
            for oc in range(2):
                tt(DVE, [ro[oc][1], ga[oc][1]], [ga[oc][1]], ga[oc][0][:], ro[oc][0][:, :], ga[oc][0][:], ALU.mult)
            slot, sB = w_get(("co", cpair * WB, 0, 16))
            co = proj2(slot, sB, act[:, 16:32, :], actB[16:32])
            for oc in range(2):
                c = 2 * cpair + oc
                tt(DVE, [co[oc][1], gb[oc][1]], [gb[oc][1]], gb[oc][0][:], co[oc][0][:, :], gb[oc][0][:], ALU.mult)
                tt(POOL, [ga[oc][1], gb[oc][1]], [actB[32 + c]], act[:, 32 + c, :], ga[oc][0][:], gb[oc][0][:], ALU.add)

        for cpair in range(8):
            out_proj_residual([("mo", cpair * WB, 0, 16)], [list(range(32, 48))], 2 * cpair, g1col)

        rmsnorm_to_xnT(a2col, sh2col)

        for fp in range(FC // 2):
            slot, sB = w_get(("gate", fp * WB, 0, 16))
            gt = proj2(slot, sB, xnT, xnB)
            tl = []
            for oc in range(2):
                f = 2 * fp + oc
                p, pB = pbuf()
                cp(POOL, [halo_ffB], [pB], p[:, 0:2], halo_ff[:, f, :])
                act_op([gt[oc][1]], [pB], out=p[:, 2:T + 2], in_=gt[oc][0][:, :], func=AF.Copy)
                cp(POOL, [pB], [halo_ffB], halo_ff[:, f, :], p[:, T:T + 2])
                tq, tqB = scr()
                act_op([pB, colsB], [tqB], out=tq[:], in_=p[:, 2:T + 2], func=AF.Identity,
                       scale=cols[:, C_WFF + 88 + f:C_WFF + 88 + f + 1], bias=cols[:, C_BFF + f:C_BFF + f + 1])
                stt(DVE, [pB, colsB, tqB], [tqB], tq[:], p[:, 1:T + 1], cols[:, C_WFF + 44 + f:C_WFF + 44 + f + 1], tq[:],
                    ALU.mult, ALU.add)
                stt(DVE, [pB, colsB, tqB], [tqB], tq[:], p[:, 0:T], cols[:, C_WFF + f:C_WFF + f + 1], tq[:],
                    ALU.mult, ALU.add)
                act_op([tqB], [tqB], out=tq[:], in_=tq[:], func=AF.Silu)
                tl.append((tq, tqB))
            slot, sB = w_get(("up", fp * WB, 0, 16))
            up = proj2(slot, sB, xnT, xnB)
            for oc in range(2):
                f = 2 * fp + oc
                tt(DVE, [up[oc][1], tl[oc][1]], [actB[f]], act[:, f, :], up[oc][0][:, :], tl[oc][0][:], ALU.mult)

        for cpair in range(8):
            sls, kcl = [], []
            for (kc0, nkc) in ((0, 16), (16, 16), (32, 12)):
                sls.append(("down", cpair * WB, kc0, nkc))
                kcl.append(list(range(kc0, kc0 + nkc)))
            out_proj_residual(sls, kcl, 2 * cpair, g2col)

        ssc, ssB_ = smcol(4)
        rsc, rsB_ = smcol(4)
        rstd, rstdB_ = smcol(4)
        for n in range(NCH):
            sl = actB[4 * n:4 * n + 4]
            act_op([hB[n]], sl + [ssB_[n]], out=rt[:, n, :], in_=h_t[:, n, :], func=AF.Square, accum_out=ssc[:, n:n + 1])
            act_op([ssB_[n]], [rsB_[n]], out=rsc[:, n:n + 1], in_=ssc[:, n:n + 1], func=AF.Sqrt, scale=1.0 / D, bias=EPS)
            emit(DVE, [rsB_[n]], [rstdB_[n]], lambda n=n: nc.vector.reciprocal(out=rstd[:, n:n + 1], in_=rsc[:, n:n + 1]))
            stt(DVE, [hB[n], rstdB_[n], fgB], [hB[n]], h_t[:, n, :], h_t[:, n, :], rstd[:, n:n + 1], fgbc[:], ALU.mult, ALU.mult)
            dma(SP, hio[n], y_d[t0 + n * 128:t0 + (n + 1) * 128, :], h_t[:, n, :], [hB[n]], [])

    for n in range(NCH):
        nc.sync.wait_ge(hio[n].h, hio[n].count)
    assert wstate["used"] == len(blocks), (wstate, len(blocks))
    return nc


_CACHE = {}


def _host_consts():
    idx = np.arange(128, dtype=np.float64)
    kdec = np.stack([np.exp(-LOG_GAMMA[h] * (idx + 1.0)) for h in range(H)], axis=1)
    epsq = np.stack([EPS * 256.0 * np.exp(-2.0 * LOG_GAMMA[h] * (idx + 1.0)) for h in range(H)], axis=1)
    invf = (np.float32(10000.0) ** (-(np.arange(128, dtype=np.float32)) / np.float32(128))).astype(np.float32)
    ident = np.eye(128, dtype=np.float32)
    j = np.arange(128)[:, None]
    i = np.arange(128)[None, :]
    maskT = (i >= j).astype(np.float32)
    return kdec.astype(np.float32), epsq.astype(np.float32), invf, ident, maskT


def _colmajor(v, nch):
    return np.ascontiguousarray(np.asarray(v, dtype=np.float32).reshape(nch, 128).T)


def kernel(x, c, positions, norm1_g, norm2_g, w_ada, b_ada, w_in, b_gate, w_sc, w_ret_o, w_conv_o, w_mix_o,
           w_up, w_gate, w_ffconv, b_ffconv, w_down, final_g):
    x = np.asarray(x, dtype=np.float32)
    B = x.shape[0]
    if "nc" not in _CACHE:
        _CACHE["nc"] = build_nc()
    nc = _CACHE["nc"]
    kdec, epsq, invf, ident, maskT = _host_consts()
    shared = {
        "w_ada": np.ascontiguousarray(np.asarray(w_ada, np.float32)[0]),
        "w_in": np.ascontiguousarray(np.asarray(w_in, np.float32)[0]),
        "w_ret_o": np.ascontiguousarray(np.asarray(w_ret_o, np.float32)[0]),
        "w_conv_o": np.ascontiguousarray(np.asarray(w_conv_o, np.float32)[0]),
        "w_mix_o": np.ascontiguousarray(np.asarray(w_mix_o, np.float32)[0]),
        "w_up": np.ascontiguousarray(np.asarray(w_up, np.float32)[0]),
        "w_gate": np.ascontiguousarray(np.asarray(w_gate, np.float32)[0]),
        "w_down": np.ascontiguousarray(np.asarray(w_down, np.float32)[0]),
        "fg": np.ascontiguousarray(np.asarray(final_g, np.float32).reshape(1, D)),
        "ident": ident,
        "maskT": maskT,
    }
    base_cols = np.zeros((128, NCOLS), np.float32)
    base_cols[:, C_N1:C_N1 + 16] = _colmajor(np.asarray(norm1_g)[0], 16)
    base_cols[:, C_N2:C_N2 + 16] = _colmajor(np.asarray(norm2_g)[0], 16)
    base_cols[:, C_BADA:C_BADA + 96] = _colmajor(np.asarray(b_ada)[0], 96)
    base_cols[:, C_BG:C_BG + 32] = _colmajor(np.asarray(b_gate)[0], 32)
    for k in range(3):
        base_cols[:, C_WSC + 16 * k:C_WSC + 16 * (k + 1)] = _colmajor(np.asarray(w_sc)[0, k], 16)
        base_cols[:, C_WFF + 44 * k:C_WFF + 44 * (k + 1)] = _colmajor(np.asarray(w_ffconv)[0, k], 44)
    base_cols[:, C_BFF:C_BFF + 44] = _colmajor(np.asarray(b_ffconv)[0], 44)
    base_cols[:, C_KDEC:C_KDEC + 8] = kdec
    base_cols[:, C_EPSQ:C_EPSQ + 8] = epsq
    base_cols[:, C_INVF] = invf
    in_maps = []
    pos = np.asarray(positions).astype(np.int32)
    for b in range(B):
        cb = base_cols.copy()
        cb[:, C_C:C_C + 16] = _colmajor(np.asarray(c)[b], 16)
        m = dict(shared)
        m["x"] = np.ascontiguousarray(x[b])
        m["pos"] = np.ascontiguousarray(pos[b].reshape(1, S))
        m["cols"] = cb
        in_maps.append(m)
    res = run_bass_kernel_spmd(nc, in_maps, core_ids=list(range(B)))
    out = np.stack([np.asarray(res.results[b]["y"], dtype=np.float32) for b in range(B)], axis=0)
    return out
```
